# Optimizing a Trainium2 kernel written in Bass

```python
import math
import jax, jax.numpy as jnp
from jax import lax
import numpy as np

D_MODEL = 2048
BATCH = 1
SEQ = 8192
DEPTH = 1

SSD_D_INNER = D_MODEL
SSD_HEAD_DIM = 64
SSD_HEADS = SSD_D_INNER // SSD_HEAD_DIM
SSD_GROUPS = 4
SSD_STATE = 128
SSD_CONV = 4
SSD_CONV_DIM = SSD_D_INNER + 2 * SSD_GROUPS * SSD_STATE
RET_HEADS = 8
RET_QK_DIM = 128
RET_V_DIM = 256
RET_QK_WIDTH = RET_HEADS * RET_QK_DIM
RET_V_WIDTH = RET_HEADS * RET_V_DIM
MIX_WIDTH = SSD_D_INNER + RET_V_WIDTH
CHUNK = 128
ROPE_BASE = 10000.0
EPS = 1e-6
OFF_Z = SSD_D_INNER
OFF_XBC = OFF_Z + SSD_CONV_DIM
OFF_DT = OFF_XBC + SSD_HEADS
OFF_Q = OFF_DT + RET_QK_WIDTH
OFF_K = OFF_Q + RET_QK_WIDTH
OFF_V = OFF_K + RET_V_WIDTH
IN_WIDTH = OFF_V + RET_V_WIDTH

kernel_name = "hymba_ssd_retention_adaln_block"


def rms_norm(x, w):
    xf = x.astype(jnp.float32)
    y = xf * lax.rsqrt(jnp.mean(xf * xf, axis=-1, keepdims=True) + EPS)
    return (y * w.astype(jnp.float32)).astype(x.dtype)


def gated_group_rms_norm(y, z, w):
    bsz, L, dn = y.shape
    u = (y * jax.nn.silu(z)).astype(jnp.float32).reshape(bsz, L, SSD_GROUPS, dn // SSD_GROUPS)
    u = u * lax.rsqrt(jnp.mean(u * u, axis=-1, keepdims=True) + EPS)
    return (u.reshape(bsz, L, dn) * w.astype(jnp.float32)).astype(y.dtype)


def head_group_norm(o, w):
    bsz, L, H, dv = o.shape
    of = o.astype(jnp.float32)
    mu = jnp.mean(of, axis=-1, keepdims=True)
    var = jnp.mean(jnp.square(of - mu), axis=-1, keepdims=True)
    y = ((of - mu) * lax.rsqrt(var + EPS)).reshape(bsz, L, H * dv)
    return (y * w.astype(jnp.float32)).astype(o.dtype)


def causal_dwconv(u, w, b):
    K = w.shape[0]
    y = lax.conv_general_dilated(u, w[:, None, :].astype(u.dtype), window_strides=(1,),
                                 padding=[(K - 1, 0)], dimension_numbers=('NWC', 'WIO', 'NWC'),
                                 feature_group_count=u.shape[-1])
    return y + b.astype(u.dtype)


def rotary(u, pos):
    d = u.shape[-1]
    half = d // 2
    inv = ROPE_BASE ** (-jnp.arange(half, dtype=jnp.float32) / half)
    ang = pos.astype(jnp.float32)[:, None] * inv[None, :]
    cos = jnp.cos(ang)[None, :, None, :]
    sin = jnp.sin(ang)[None, :, None, :]
    uf = u.astype(jnp.float32)
    u1, u2 = uf[..., :half], uf[..., half:]
    return jnp.concatenate([u1 * cos - u2 * sin, u1 * sin + u2 * cos], axis=-1).astype(u.dtype)


def ssd_chunked(xh, dt, a, bm, cm):
    bsz, L, H, P = xh.shape
    G, N = bm.shape[2], bm.shape[3]
    R = H // G
    nc = L // CHUNK
    dtype = xh.dtype
    x = xh.reshape(bsz, nc, CHUNK, G, R, P)
    dt = dt.reshape(bsz, nc, CHUNK, G, R)
    bm = bm.reshape(bsz, nc, CHUNK, G, N)
    cm = cm.reshape(bsz, nc, CHUNK, G, N)
    a_cs = jnp.cumsum(dt * a.reshape(G, R), axis=2)
    causal = jnp.tril(jnp.ones((CHUNK, CHUNK), dtype=bool))
    seg = a_cs[:, :, :, None] - a_cs[:, :, None, :]
    decay = jnp.exp(jnp.where(causal[:, :, None, None], seg, -jnp.inf)).astype(dtype)
    xdt = x * dt.astype(dtype)[..., None]
    cb = jnp.einsum('bctgn,bcsgn->bcgts', cm, bm)
    y_diag = jnp.einsum('bcgts,bctsgr,bcsgrp->bctgrp', cb, decay, xdt)
    decay_to_end = jnp.exp(a_cs[:, :, -1:] - a_cs).astype(dtype)
    states = jnp.einsum('bclgn,bclgr,bclgrp->bcgrpn', bm, decay_to_end, xdt)
    chunk_decay = jnp.exp(a_cs[:, :, -1])

    def step(carry, inp):
        st, dec = inp
        new = carry * dec[..., None, None] + st
        return new, carry

    init = jnp.zeros((bsz, G, R, P, N), jnp.float32)
    _, prev = lax.scan(step, init, (jnp.moveaxis(states.astype(jnp.float32), 1, 0),
                                     jnp.moveaxis(chunk_decay, 1, 0)))
    prev = jnp.moveaxis(prev, 0, 1).astype(dtype)
    y_off = jnp.einsum('bctgn,bcgrpn,bctgr->bctgrp', cm, prev, jnp.exp(a_cs).astype(dtype))
    return (y_diag + y_off).reshape(bsz, L, H, P)


def retention_chunked(q, k, v, log_gamma):
    bsz, L, H, Dk = q.shape
    Dv = v.shape[-1]
    nc = L // CHUNK
    dtype = q.dtype
    q = q.reshape(bsz, nc, CHUNK, H, Dk)
    k = k.reshape(bsz, nc, CHUNK, H, Dk)
    v = v.reshape(bsz, nc, CHUNK, H, Dv)
    idx = jnp.arange(CHUNK, dtype=jnp.float32)
    diff = idx[:, None] - idx[None, :]
    dmask = jnp.where((diff >= 0)[None], jnp.exp(diff[None] * log_gamma[:, None, None]), 0.0)
    scores = jnp.einsum('bcihd,bcjhd->bchij', q, k) * dmask.astype(dtype)
    o_intra = jnp.einsum('bchij,bcjhe->bcihe', scores, v)
    zeta = jnp.exp((CHUNK - 1 - idx)[None, :] * log_gamma[:, None])
    kv = jnp.einsum('bcjhd,hj,bcjhe->bchde', k, zeta.astype(dtype), v)
    chunk_decay = jnp.exp(CHUNK * log_gamma)[None, :, None, None]

    def step(carry, st):
        return carry * chunk_decay + st, carry

    init = jnp.zeros((bsz, H, Dk, Dv), jnp.float32)
    _, prev = lax.scan(step, init, jnp.moveaxis(kv.astype(jnp.float32), 1, 0))
    prev = jnp.moveaxis(prev, 0, 1).astype(dtype)
    xi = jnp.exp((idx + 1.0)[:, None] * log_gamma[None, :])
    o_cross = jnp.einsum('bcihd,bchde,ih->bcihe', q, prev, xi.astype(dtype))
    return (o_intra + o_cross).reshape(bsz, L, H, Dv)


def setup_inputs(seed: int = 0) -> dict:
    key = jax.random.key(seed)
    ks = jax.random.split(key, 16)
    f32 = jnp.float32
    x = jax.random.normal(ks[0], (BATCH, SEQ, D_MODEL), f32)
    c = jax.random.normal(ks[1], (BATCH, D_MODEL), f32)
    w_ada = jax.random.normal(ks[2], (DEPTH, D_MODEL, 3 * D_MODEL), f32) * (0.5 * D_MODEL ** -0.5)
    b_ada = 0.01 * jax.random.normal(ks[3], (DEPTH, 3 * D_MODEL), f32)
    norm_w = 1.0 + 0.02 * jax.random.normal(ks[4], (DEPTH, D_MODEL), f32)
    w_in = jax.random.normal(ks[5], (DEPTH, D_MODEL, IN_WIDTH), f32) * D_MODEL ** -0.5
    conv_w = jax.random.normal(ks[6], (DEPTH, SSD_CONV, SSD_CONV_DIM), f32) * SSD_CONV ** -0.5
    conv_b = 0.01 * jax.random.normal(ks[7], (DEPTH, SSD_CONV_DIM), f32)
    dt0 = jnp.exp(jax.random.uniform(ks[8], (DEPTH, SSD_HEADS), f32,
                                     math.log(1e-3), math.log(1e-1)))
    dt_bias = dt0 + jnp.log(-jnp.expm1(-dt0))
    a_log = jnp.log(jax.random.uniform(ks[9], (DEPTH, SSD_HEADS), f32, 1.0, 16.0))
    d_skip = 1.0 + 0.1 * jax.random.normal(ks[10], (DEPTH, SSD_HEADS), f32)
    ssd_norm_w = 1.0 + 0.02 * jax.random.normal(ks[11], (DEPTH, SSD_D_INNER), f32)
    ret_norm_w = 1.0 + 0.02 * jax.random.normal(ks[12], (DEPTH, RET_V_WIDTH), f32)
    w_out = jax.random.normal(ks[13], (DEPTH, MIX_WIDTH, D_MODEL), f32) * MIX_WIDTH ** -0.5
    final_norm_w = 1.0 + 0.02 * jax.random.normal(ks[14], (D_MODEL,), f32)
    return {"x": x, "c": c, "w_ada": w_ada, "b_ada": b_ada, "norm_w": norm_w, "w_in": w_in,
            "conv_w": conv_w, "conv_b": conv_b, "dt_bias": dt_bias, "a_log": a_log,
            "d_skip": d_skip, "ssd_norm_w": ssd_norm_w, "ret_norm_w": ret_norm_w,
            "w_out": w_out, "final_norm_w": final_norm_w}


def reference(x, c, w_ada, b_ada, norm_w, w_in, conv_w, conv_b, dt_bias, a_log, d_skip,
              ssd_norm_w, ret_norm_w, w_out, final_norm_w):
    bsz, L, _ = x.shape
    pos = jnp.arange(L, dtype=jnp.int32)
    log_gamma = jnp.log1p(-jnp.exp2(-5.0 - jnp.arange(RET_HEADS, dtype=jnp.float32)))
    cond = jax.nn.silu(c)
    h = x
    for layer in range(DEPTH):
        mod = cond @ w_ada[layer] + b_ada[layer]
        shift, scale, gate = jnp.split(mod, 3, axis=-1)
        u = rms_norm(h, norm_w[layer]) * (1.0 + scale[:, None, :]) + shift[:, None, :]
        proj = u @ w_in[layer]
        z, xbc, dt_raw, q, k, v, g = jnp.split(proj, [OFF_Z, OFF_XBC, OFF_DT, OFF_Q, OFF_K, OFF_V], axis=-1)

        xbc = jax.nn.silu(causal_dwconv(xbc, conv_w[layer], conv_b[layer]))
        xs, bm, cm = jnp.split(xbc, [SSD_D_INNER, SSD_D_INNER + SSD_GROUPS * SSD_STATE], axis=-1)
        xs = xs.reshape(bsz, L, SSD_HEADS, SSD_HEAD_DIM)
        bm = bm.reshape(bsz, L, SSD_GROUPS, SSD_STATE)
        cm = cm.reshape(bsz, L, SSD_GROUPS, SSD_STATE)
        dt = jax.nn.softplus(dt_raw.astype(jnp.float32) + dt_bias[layer].astype(jnp.float32))
        a = -jnp.exp(a_log[layer].astype(jnp.float32))
        y = ssd_chunked(xs, dt, a, bm, cm) + xs * d_skip[layer][:, None]
        y_ssd = gated_group_rms_norm(y.reshape(bsz, L, SSD_D_INNER), z, ssd_norm_w[layer])

        qh = rotary(q.reshape(bsz, L, RET_HEADS, RET_QK_DIM), pos)
        kh = rotary(k.reshape(bsz, L, RET_HEADS, RET_QK_DIM), pos) * (RET_QK_DIM ** -0.5)
        vh = v.reshape(bsz, L, RET_HEADS, RET_V_DIM)
        o = retention_chunked(qh, kh, vh, log_gamma)
        y_ret = head_group_norm(o, ret_norm_w[layer]) * jax.nn.silu(g)

        mixed = jnp.concatenate([y_ssd, y_ret], axis=-1) @ w_out[layer]
        h = h + gate[:, None, :] * mixed
    return rms_norm(h, final_norm_w)
```

```python
import contextlib
import math
import numpy as np
import ml_dtypes
import concourse.bass as bass
import concourse.mybir as mybir
from concourse.bass_utils import run_bass_kernel_spmd

F32 = mybir.dt.float32
BF16 = mybir.dt.bfloat16
AF = mybir.ActivationFunctionType
ALU = mybir.AluOpType
NCORE = 8
T = 1024
D = 2048
INW = 11296
EPS = 1e-6
O_XBC, O_DT, O_Q, O_K, O_V, O_G = 2048, 5120, 5152, 6176, 7200, 9248
LG = [math.log1p(-2.0 ** (-5.0 - h)) for h in range(8)]
G128 = [math.exp(128.0 * l) for l in LG]
C3 = [math.exp(-128.0 * l) for l in LG]
NCST = 592
NPC = 73
SGC = False

ENGS = ("pe", "act", "dve", "pool", "sp")


class Buf:
    __slots__ = ("name", "w", "r")

    def __init__(self, name, after=None):
        self.name = name
        self.w = None
        self.r = list(after) if after else []


def retire(bufs):
    out = []
    for b in bufs:
        if b.w is not None:
            out.append(b.w)
        out.extend(b.r)
    m = {}
    for k, v in out:
        if m.get(k, 0) < v:
            m[k] = v
    return list(m.items())


class DmaSem:
    def __init__(self, sched, name):
        self.key = "dma_" + name
        self.count = 0
        sched.semkeys.append(self.key)


class Sched:
    def __init__(self):
        self.prog = {e: [] for e in ENGS}
        self.cnt = {e: 0 for e in ENGS}
        self.semkeys = list(ENGS)
        self.waited = {}

    def dma_sem(self, name):
        return DmaSem(self, name)

    def _deps(self, eng, reads, writes):
        deps = {}

        def add(t):
            k, v = t
            if deps.get(k, 0) < v:
                deps[k] = v

        for b in reads:
            if b.w is not None:
                add(b.w)
        for b in writes:
            if b.w is not None:
                add(b.w)
            for t in b.r:
                add(t)
        for k, v in deps.items():
            if k == "pe" and eng == "pe":
                continue
            if self.waited.get((eng, k), 0) >= v:
                continue
            self.waited[(eng, k)] = v
            self.prog[eng].append(("wait", k, v))

    def _mark(self, ticket, reads, writes):
        for b in reads:
            b.r.append(ticket)
            if len(b.r) > 16:
                m = {}
                for k, v in b.r:
                    if m.get(k, 0) < v:
                        m[k] = v
                b.r = list(m.items())
        for b in writes:
            b.w = ticket
            b.r = []

    def op(self, eng, fn, reads=(), writes=(), inc=True):
        self._deps(eng, reads, writes)
        if inc:
            self.cnt[eng] += 1
            self.prog[eng].append(("op", fn, eng, 1))
            self._mark((eng, self.cnt[eng]), reads, writes)
        else:
            self.prog[eng].append(("op", fn, None, 0))
            self._mark((eng, self.cnt[eng] + 1), reads, writes)

    def dma(self, q, sem, fn, reads=(), writes=(), inc=16):
        self._deps(q, reads, writes)
        sem.count += inc
        self.prog[q].append(("op", fn, sem.key, inc))
        self._mark((sem.key, sem.count), reads, writes)

    def wait(self, eng, ticket):
        k, v = ticket
        if self.waited.get((eng, k), 0) >= v:
            return
        self.waited[(eng, k)] = v
        self.prog[eng].append(("wait", k, v))

    def emit(self, nc):
        with contextlib.ExitStack() as st:
            sems = {k: st.enter_context(nc.semaphore("s_" + k)) for k in self.semkeys}
            block = st.enter_context(nc.Block())
            prog = self.prog

            def run(e, eng):
                for item in prog[e]:
                    if item[0] == "wait":
                        eng.wait_ge(sems[item[1]], item[2])
                    else:
                        _, fn, sk, inc = item
                        ins = fn(eng)
                        if sk is not None:
                            ins.then_inc(sems[sk], inc)

            block.tensor(lambda eng: run("pe", eng))
            block.scalar(lambda eng: run("act", eng))
            block.vector(lambda eng: run("dve", eng))
            block.gpsimd(lambda eng: run("pool", eng))
            block.sync(lambda eng: run("sp", eng))


def build_program(debug=None):
    nc = bass.Bass("TRN2", target_bir_lowering=False)
    s = Sched()

    def din(name, shape, dt=F32):
        return nc.dram_tensor(name, list(shape), dt, kind="ExternalInput").ap()

    x_d = din("x", [T, D])
    xh_d = din("xh", [3, D])
    win_d = din("w_in", [D, INW])
    wout_d = din("w_out", [4096, D])
    wada_d = din("w_ada", [D, 768])
    bada_d = din("b_ada", [1, 768])
    cst_d = din("cst", [128, NCST])
    cstb_d = din("cstb", [128, 640])
    pc_d = din("pc", [128, NPC])
    pp_d = din("pp", [128, 152])
    bv_d = din("bv", [1, 96])
    rot_d = din("rot", [T, 256])
    wssd_d = din("ssd_norm_w", [1, D])
    wret_d = din("ret_norm_w", [1, D])
    wfin_d = din("final_norm_w", [1, D])
    out_d = nc.dram_tensor("out", [T, D], F32, kind="ExternalOutput").ap()
    dbg_d = None
    if debug:
        dbg_d = nc.dram_tensor("dbg", [128, debug[1]], F32, kind="ExternalOutput").ap()
    m_in = nc.dram_tensor("m_in", [1, 768], F32)
    m_out = nc.dram_tensor("m_out", [NCORE, 768], F32)
    a_in = nc.dram_tensor("a_in", [128, 2080], F32)
    a_out = nc.dram_tensor("a_out", [NCORE * 128, 2080], F32)
    r_in = nc.dram_tensor("r_in", [128, 2048], F32)
    r_out = nc.dram_tensor("r_out", [NCORE * 128, 2048], F32)

    st = contextlib.ExitStack()
    with st:
        def sb(name, shape, dt):
            return st.enter_context(nc.sbuf_tensor("sb_" + name, list(shape), dt))

        uT = sb("uT", [128, 16, 1032], BF16)
        Wb = sb("Wb", [128, 2, 16 * 512], BF16)
        XS = sb("XS", [128, 8, 2048], BF16)
        VT = sb("VT", [128, 8, 2048], BF16)
        BK = sb("BK", [128, 8, 1024], BF16)
        AUX = sb("AUX", [128, 8192], BF16)
        SCR = sb("SCR", [128, 10240], F32)
        cst = sb("cst", [128, NCST], F32)
        cstb = sb("cstb", [128, 640], BF16)
        pc = sb("pc", [128, NPC], F32)
        pp = sb("pp", [128, 152], F32)
        bv = sb("bv", [128, 96], F32)
        sm = sb("sm", [128, 256], F32)
        PF = [st.enter_context(nc.psum_tensor("pf%d" % i, [128, 512], F32)) for i in range(6)]
        PT = [st.enter_context(nc.psum_tensor("pt%d" % i, [128, 1024], BF16)) for i in range(2)]
        bPF = [Buf("pf%d" % i) for i in range(6)]
        bPT = [Buf("pt%d" % i) for i in range(2)]

        ident = cst[:, 0:128]
        triU = cst[:, 128:256]
        triLs = cst[:, 256:384]
        ones = cst[:, 384:512]
        zk = cst[:, 512:520]
        xi = cst[:, 520:528]
        zz = cst[:, 528:592]
        identb = cstb[:, 0:128]
        negm4 = cstb[:, 128:640]
        bcst = Buf("cst")

        dsem = {}

        def DS(name):
            if name not in dsem:
                dsem[name] = s.dma_sem(name)
            return dsem[name]

        def act(out, in_, func, reads, writes, bias=0.0, scale=1.0, accum=None):
            if accum is None:
                s.op("act", lambda e: e.activation(out=out, in_=in_, func=func, bias=bias, scale=scale), reads, writes)
            else:
                s.op("act", lambda e: e.activation(out=out, in_=in_, func=func, bias=bias, scale=scale, accum_out=accum), reads, writes)

        def tt(eng, out, in0, in1, op, reads, writes):
            s.op(eng, lambda e: e.tensor_tensor(out=out, in0=in0, in1=in1, op=op), reads, writes)

        def ts(eng, out, in0, s1, s2, op0, op1, reads, writes):
            if s2 is None:
                s.op(eng, lambda e: e.tensor_scalar(out=out, in0=in0, scalar1=s1, scalar2=None, op0=op0), reads, writes)
            else:
                s.op(eng, lambda e: e.tensor_scalar(out=out, in0=in0, scalar1=s1, scalar2=s2, op0=op0, op1=op1), reads, writes)

        def stt(eng, out, in0, scalar, in1, op0, op1, reads, writes):
            s.op(eng, lambda e: e.scalar_tensor_tensor(out=out, in0=in0, scalar=scalar, in1=in1, op0=op0, op1=op1), reads, writes)

        def cp(eng, out, in_, reads, writes):
            if eng == "act":
                act(out, in_, AF.Copy, reads, writes)
            else:
                s.op(eng, lambda e: e.tensor_copy(out=out, in_=in_), reads, writes)

        def mm(out, lhsT, rhs, start, stop, reads, writes, inc=True, sgc=False):
            if sgc:
                s.op("pe", lambda e: e.matmul(out, lhsT=lhsT, rhs=rhs, start=start, stop=stop, skip_group_check=SGC), reads, writes, inc=inc)
            else:
                s.op("pe", lambda e: e.matmul(out, lhsT=lhsT, rhs=rhs, start=start, stop=stop), reads, writes, inc=inc)

        def tr(out, in_, idn, reads, writes, inc=True):
            s.op("pe", lambda e: e.transpose(out=out, in_=in_, identity=idn), reads, writes, inc=inc)

        def dma(q, semname, out, in_, reads, writes, slow=False):
            if slow:
                s.dma(q, DS(semname), lambda e: e.dma_start(out=out, in_=in_, allow_slow_non_contiguous=True), reads, writes)
            else:
                s.dma(q, DS(semname), lambda e: e.dma_start(out=out, in_=in_), reads, writes)

        def allgather(semname, src, dst, reads, writes):
            sem = DS(semname)
            s.dma("pool", sem, lambda e: e.collective_compute(
                "AllGather", ALU.bypass, replica_groups=[list(range(NCORE))],
                ins=[src.ap().opt()], outs=[dst.ap().opt()]), reads, writes, inc=1)

        def rsqrt_act(out, in_, scale, reads_writes_buf):
            act(out, in_, AF.Ln, [reads_writes_buf], [reads_writes_buf], bias=epsb, scale=scale)
            act(out, out, AF.Exp, [reads_writes_buf], [reads_writes_buf], scale=-0.5)

        dma("sp", "c0", cst[:], cst_d[:, :], [], [bcst])
        dma("sp", "c0", pc[:], pc_d[:, :], [], [bcst])
        dma("sp", "c0", pp[:], pp_d[:, :], [], [bcst])
        dma("sp", "c0", bv[:], bv_d.partition_broadcast(128), [], [bcst])
        dma("pool", "c1", cstb[:], cstb_d[:, :], [], [bcst])
        bsm = Buf("sm")
        s.op("dve", lambda e: e.memset(sm[:], 0.0), [], [bsm])
        epsb = sm[:, 255:256]
        s.op("dve", lambda e: e.memset(sm[:, 255:256], EPS), [bsm], [bsm])
        nw = pp[:, 0:16]
        cvec = pp[:, 16:32]
        convw = pp[:, 32:128].rearrange("p (c k) -> p c k", k=4)
        convb = pp[:, 128:152]
        hflag = pc[:, 0:1]

        XSf = XS[:].rearrange("p a b -> p (a b)").bitcast(F32)
        VTf = VT[:].rearrange("p a b -> p (a b)").bitcast(F32)
        BKf = BK[:].rearrange("p a b -> p (a b)").bitcast(F32)
        AUXf = AUX[:].bitcast(F32)
        xt = [XSf[:, i * 2048:(i + 1) * 2048] for i in range(4)] + [VTf[:, i * 2048:(i + 1) * 2048] for i in range(4)]
        Wslot1 = Wb[:, 1, :]
        xth = Wslot1[:, 2048:6144].bitcast(F32)
        bxt = [Buf("xt%d" % i) for i in range(8)]
        bxth = Buf("xth")
        dma("act", "xa", xth[0:3, :], xh_d[:, :], [], [bxth])
        for m in range(8):
            dma("act", "x" + "abcd"[m // 2], xt[m], x_d[m * 128:(m + 1) * 128, :], [], [bxt[m]])
        for grp, bufs in (("xa", [bxth, bxt[0], bxt[1]]), ("xb", [bxt[2], bxt[3]]), ("xc", [bxt[4], bxt[5]]), ("xd", [bxt[6], bxt[7]])):
            for b_ in bufs:
                b_.w = (DS(grp).key, DS(grp).count)
        bwa = [Buf("wada0"), Buf("wada1")]
        wa = [SCR[:, 4096 + i * 3072:4096 + (i + 1) * 3072].rearrange("p (k n) -> p k n", k=16) for i in range(2)]
        bacc = Buf("macc")
        macc = SCR[:, 0:768]
        cond = sm[:, 0:16]
        act(cond, cvec, AF.Silu, [bcst, bsm], [bsm])
        for qd in range(4):
            w_ = wa[qd % 2]
            dma("sp", "wa%d" % (qd % 2), w_, wada_d[:, qd * 192:(qd + 1) * 192].rearrange("(k p) n -> p k n", p=128), [], [bwa[qd % 2]])
            o = macc[:, qd * 192:(qd + 1) * 192]
            for k in range(16):
                if k == 0:
                    ts("dve", o, w_[:, 0, :], cond[:, 0:1], None, ALU.mult, ALU.bypass, [bwa[qd % 2], bsm], [bacc])
                else:
                    stt("dve", o, w_[:, k, :], cond[:, k:k + 1], o, ALU.mult, ALU.add, [bwa[qd % 2], bsm, bacc], [bacc])
        mm(PF[2][0:1, 0:512], ones[:, 0:1], macc[:, 0:512], True, True, [bcst, bacc], [bPF[2]])
        mm(PF[3][0:1, 0:256], ones[:, 0:1], macc[:, 512:768], True, True, [bcst, bacc], [bPF[3]])
        bmrow = Buf("mrow")
        mrow = SCR[0:1, 768:1536]
        brow = SCR[0:1, 1536:2304]
        dma("sp", "one", brow, bada_d[:, :], [], [bmrow])
        tt("dve", mrow[:, 0:512], PF[2][0:1, 0:512], brow[:, 0:512], ALU.add, [bPF[2], bmrow], [bmrow])
        tt("dve", mrow[:, 512:768], PF[3][0:1, 0:256], brow[:, 512:768], ALU.add, [bPF[3], bmrow], [bmrow])
        bmin = Buf("m_in"); bmout = Buf("m_out")
        dma("sp", "one", m_in.ap(), mrow, [bmrow], [bmin])
        allgather("ag0", m_in, m_out, [bmin], [bmout])
        modf = m_out.ap().rearrange("r n -> (r n)")
        bmod = Buf("mod")
        shiftP = sm[:, 16:32]
        scaleP = sm[:, 32:48]
        sP = sm[:, 48:64]
        ssq = sm[:, 64:80]
        bssq = Buf("ssq")

        buT = [Buf("uT%d" % m) for m in range(8)]
        buTh = Buf("uTh")
        xn = [BK[:, 2 * j:2 * j + 2, :].rearrange("p a b -> p (a b)") for j in range(4)] + [AUX[:, j * 2048:(j + 1) * 2048] for j in range(4)]
        xn.append(Wslot1[:, 6144:8192])
        bxn = [Buf("xn%d" % j) for j in range(9)]
        junk = Wslot1[:, 0:2048]
        bjunk = Buf("junk")

        def norm_stats(xtile, bx, npart, col, j):
            act(junk[0:npart, :], xtile[0:npart, :], AF.Square, [bx, bsm], [bjunk, bssq], accum=ssq[0:npart, col:col + 1])
            act(ssq[0:npart, col:col + 1], ssq[0:npart, col:col + 1], AF.Ln, [bssq, bsm], [bssq], bias=epsb[0:npart, :], scale=1.0 / D)
            act(ssq[0:npart, col:col + 1], ssq[0:npart, col:col + 1], AF.Exp, [bssq], [bssq], scale=-0.5)
            act(xn[j][0:npart, :], xtile[0:npart, :], AF.Identity, [bx, bssq], [bxn[j]], scale=ssq[0:npart, col:col + 1])

        def norm_tr(npart, j, half):
            for k in range(8):
                kc = half * 8 + k
                tr(PT[half][:, k * 128:k * 128 + npart], xn[j][0:npart, kc * 128:(kc + 1) * 128], identb[0:npart, 0:npart],
                   [bxn[j], bcst], [bPT[half]], inc=(k == 7))

        def norm_evac(npart, half, dst_cols, bdst):
            for k in range(8):
                kc = half * 8 + k
                o_ = uT[:, kc, dst_cols[0]:dst_cols[1]]
                i_ = PT[half][:, k * 128:k * 128 + npart]
                if k % 2 == 0:
                    ts("dve", o_, i_, sP[:, kc:kc + 1], shiftP[:, kc:kc + 1], ALU.mult, ALU.add, [bPT[half], bmod], [bdst])
                else:
                    act(o_, i_, AF.Identity, [bPT[half], bmod], [bdst], bias=shiftP[:, kc:kc + 1], scale=sP[:, kc:kc + 1])

        norm_stats(xth, bxth, 3, 8, 8)
        for m in range(8):
            norm_stats(xt[m], bxt[m], 128, m, m)
        dma("sp", "one", shiftP, modf[0:2048].rearrange("(p k) -> p k", k=16), [bmout, bsm], [bmod])
        dma("sp", "one", scaleP, modf[2048:4096].rearrange("(p k) -> p k", k=16), [bmout, bsm], [bmod])
        stt("dve", sP, scaleP, 1.0, nw, ALU.add, ALU.mult, [bmod, bcst], [bmod])
        for half in range(2):
            norm_tr(3, 8, half)
            norm_evac(3, half, (0, 3), buTh)
        ts("dve", uT[:, :, 0:3], uT[:, :, 0:3], hflag, None, ALU.mult, ALU.bypass, [buTh, bcst], [buTh])
        for m in range(8):
            for half in range(2):
                norm_tr(128, m, half)
                norm_evac(128, half, (3 + m * 128, 3 + (m + 1) * 128), buT[m])
        aft0 = retire(bwa + [bacc, bmrow, bxth, bjunk] + bxn + bxt)
        if debug and debug[0] == "p0":
            bd_ = Buf("dbgs", aft0)
            for kc in range(2):
                act(SCR[:, kc * 1032:kc * 1032 + 1027], uT[:, kc * 8, 0:1027], AF.Copy, buT + [buTh], [bd_])
            for kc in range(2):
                dma("sp", "dbg", dbg_d[:, kc * 1032:kc * 1032 + 1027], SCR[:, kc * 1032:kc * 1032 + 1027], [bd_], [Buf("dbgo%d" % kc)])
            dma("sp", "out", out_d[0:128, :], xt[0], [bxt[0]], [Buf("outx")])
            s.wait("sp", (DS("dbg").key, DS("dbg").count))
            s.wait("sp", (DS("out").key, DS("out").count))
            s.emit(nc)
            return nc
        buT_all = buT + [buTh]

        bW = [Buf("W0"), Buf("W1", aft0)]
        wlist = []
        for blk in range(6):
            wlist.append((win_d[:, O_XBC + blk * 512:O_XBC + (blk + 1) * 512], 512))
        wlist.append((win_d[:, O_DT:O_DT + 32], 32))
        zsrc = lambda g: (win_d[:, g * 512:(g + 1) * 512], 512)
        vsrc = lambda b: (win_d[:, O_V + b * 512:O_V + (b + 1) * 512], 512)
        wlist += [zsrc(0), zsrc(1), vsrc(0), vsrc(1), zsrc(2), vsrc(2), zsrc(3), vsrc(3)]
        for blk in range(2):
            wlist.append((win_d[:, O_K + blk * 512:O_K + (blk + 1) * 512], 512))
        for qt in range(4):
            wlist.append((win_d[:, O_Q + qt * 256:O_Q + (qt + 1) * 256], 256))
            wlist.append((win_d[:, O_G + qt * 512:O_G + (qt + 1) * 512], 512))
        wstate = {"issued": 0, "got": 0}

        def w_issue():
            i = wstate["issued"]
            if i >= len(wlist):
                return
            src_ap, ncols = wlist[i]
            slot = i % 2
            dst = Wb[:, slot, 0:16 * ncols].rearrange("p (k n) -> p k n", k=16)
            dma("pool", "w%d" % slot, dst, src_ap.rearrange("(k p) n -> p k n", p=128), [], [bW[slot]])
            wstate["issued"] += 1

        def load_w(ncols=None, prefetch=True):
            i = wstate["got"]
            wstate["got"] += 1
            while wstate["issued"] <= i:
                w_issue()
            if prefetch and wstate["issued"] == i + 1:
                w_issue()
            nco = wlist[i][1]
            assert ncols is None or ncols == nco, (i, ncols, nco)
            slot = i % 2
            return Wb[:, slot, 0:16 * nco].rearrange("p (k n) -> p k n", k=16), bW[slot]

        pfstate = {"n": 0}

        def next_pf():
            i = pfstate["n"] % 2
            pfstate["n"] += 1
            return PF[i], bPF[i]

        def inproj_tok(wt, bw, ncols, m):
            ps, bps = next_pf()
            for kc in range(16):
                mm(ps[:, 0:ncols], uT[:, kc, 3 + m * 128:3 + (m + 1) * 128], wt[:, kc, :], kc == 0, kc == 15,
                   [buT[m], bw], [bps], inc=(kc == 15))
            return ps, bps

        SIL = AF.Silu
        import collections
        fill_q = collections.deque()

        def pump(n):
            for _ in range(n):
                if fill_q:
                    fill_q.popleft()()

        def drain():
            while fill_q:
                fill_q.popleft()()

        def tok_block(ncols, evac, q=None, prefetch=True):
            st_ = {}
            q = fill_q if q is None else q

            def tile(m):
                if "wt" not in st_:
                    st_["wt"], st_["bw"] = load_w(ncols, prefetch=prefetch)
                ps, bps = inproj_tok(st_["wt"], st_["bw"], ncols, m)
                evac(ps, bps, m)
            for m in range(8):
                q.append(lambda m=m: tile(m))

        ARENA0 = 4096
        arena = {"o": ARENA0}

        def aa(n32, dt=F32):
            o = arena["o"]
            arena["o"] += n32
            assert arena["o"] <= 10240, arena["o"]
            ap = SCR[:, o:o + n32]
            return ap if dt == F32 else ap.bitcast(BF16)

        bXS = [[Buf("XS%d_%d" % (c, g), aft0) for g in range(4)] for c in range(8)]
        bBK = [Buf("BK%d" % i, aft0) for i in range(8)]
        bVT = [[Buf("VT%d_%d" % (c, h), aft0) for h in range(8)] for c in range(8)]

        raw = [aa(1032), aa(1032)]
        cacc = [aa(1024), aa(1024)]
        xf = [aa(512, BF16), aa(512, BF16)]
        braw = [Buf("raw0", aft0), Buf("raw1", aft0)]; bcacc = [Buf("cacc0", aft0), Buf("cacc1", aft0)]; bxf = [Buf("xf0", aft0), Buf("xf1", aft0)]
        bstate = {"pc": 0, "hl": 0}

        def B_front_pe(ct, wt, bw):
            ci = ct % 4
            lhs = lambda kc: wt[:, kc, ci * 128:(ci + 1) * 128]
            banks = []
            for piece in range(2):
                bi = bstate["pc"] % 4
                bstate["pc"] += 1
                ps, bps = PF[bi], bPF[bi]
                for kc in range(16):
                    mm(ps[:, 0:512], lhs(kc), uT[:, kc, 3 + piece * 512:3 + (piece + 1) * 512], kc == 0, kc == 15,
                       buT[piece * 4:(piece + 1) * 4] + [bw], [bps], inc=(kc == 15))
                banks.append((ps, bps))
            hb = 4 + bstate["hl"] % 2
            bstate["hl"] += 1
            for kc in range(16):
                mm(PF[hb][:, 0:3], lhs(kc), uT[:, kc, 0:3], kc == 0, kc == 15, [buTh, bw], [bPF[hb]], inc=(kc == 15))
            banks.append((PF[hb], bPF[hb]))
            return banks

        def B_front_ew(ct, banks):
            j = ct % 2
            for piece in range(2):
                ps, bps = banks[piece]
                cp("act", raw[j][:, 3 + piece * 512:3 + (piece + 1) * 512], ps[:, 0:512], [bps], [braw[j]])
            ps, bps = banks[2]
            cp("act", raw[j][:, 0:3], ps[:, 0:3], [bps], [braw[j]])
            act(cacc[j], raw[j][:, 0:1024], AF.Identity, [braw[j], bcst], [bcacc[j]], bias=convb[:, ct:ct + 1], scale=convw[:, ct, 0:1])
            for k in range(1, 4):
                stt("dve", cacc[j], raw[j][:, k:1024 + k], convw[:, ct, k:k + 1], cacc[j], ALU.mult, ALU.add,
                    [braw[j], bcst, bcacc[j]], [bcacc[j]])

        def B_back(ct):
            j = ct % 2
            if ct < 16:
                act(xf[j], cacc[j], SIL, [bcacc[j]], [bxf[j]])
                for c in range(8):
                    tr(PT[j][:, c * 128:(c + 1) * 128], xf[j][:, c * 128:(c + 1) * 128], identb, [bxf[j], bcst], [bPT[j]], inc=(c == 7))
                g = ct // 4
                cp("dve", XS[:, :, ct * 128:(ct + 1) * 128], PT[j][:].rearrange("p (c k) -> p c k", c=8), [bPT[j]],
                   [bXS[c][g] for c in range(8)])
            else:
                act(BK[:, ct - 16, :], cacc[j], SIL, [bcacc[j]], [bBK[ct - 16]])

        for ct in range(24):
            if ct % 4 == 0:
                wtB, bwB = load_w(512)
            banks = B_front_pe(ct, wtB, bwB)
            if ct >= 1:
                B_back(ct - 1)
            B_front_ew(ct, banks)
        B_back(23)
        aftB = retire(braw + bcacc + bxf)

        bdtf = Buf("dtf", aft0)
        A0 = SCR[:, 0:2048]

        def dtf(i):
            return A0[:, i * 256:(i + 1) * 256].rearrange("p (c h) -> p c h", c=8)

        dt_, lndt, dA, nb, E_, wch, wcore, etot = [dtf(i) for i in range(8)]
        wt, bw = load_w(32)
        for m in range(8):
            for kc in range(16):
                mm(PF[2][:, m * 32:(m + 1) * 32], uT[:, kc, 3 + m * 128:3 + (m + 1) * 128], wt[:, kc, :], kc == 0, kc == 15,
                   [buT[m], bw], [bPF[2]], inc=(kc == 15))
        dtb = bv[:, 0:32]
        alog = bv[:, 32:64]
        dskip = bv[:, 64:96]
        xr = dA
        tt("dve", xr, PF[2][:, 0:256].rearrange("p (c h) -> p c h", c=8), dtb.unsqueeze(1).broadcast_to([128, 8, 32]), ALU.add,
           [bPF[2], bcst], [bdtf])
        act(nb, xr, AF.Abs, [bdtf], [bdtf])
        act(nb, nb, AF.Exp, [bdtf], [bdtf], scale=-1.0)
        act(nb, nb, AF.Ln, [bdtf], [bdtf], bias=1.0)
        stt("dve", dt_, xr, 0.0, nb, ALU.max, ALU.add, [bdtf], [bdtf])
        act(lndt, dt_, AF.Ln, [bdtf], [bdtf])
        aexp = sm[:, 96:128]
        act(aexp, alog, AF.Exp, [bcst, bsm], [bsm])
        stt("dve", dA, dt_, -1.0, aexp.unsqueeze(1).broadcast_to([128, 8, 32]), ALU.mult, ALU.mult, [bdtf, bsm], [bdtf])
        for c in range(8):
            mm(PF[3][:, c * 32:(c + 1) * 32], triU, dA[:, c, :], True, True, [bcst, bdtf], [bPF[3]], inc=False)
            mm(PF[3][:, 256 + c * 32:256 + (c + 1) * 32], triLs, dA[:, c, :], True, True, [bcst, bdtf], [bPF[3]], inc=False)
            mm(PF[4][:, c * 32:(c + 1) * 32], ones, dA[:, c, :], True, True, [bcst, bdtf], [bPF[4]], inc=(c == 7))
        v3 = lambda lo: PF[3][:, lo:lo + 256].rearrange("p (c h) -> p c h", c=8)
        tot = wcore
        cp("dve", tot, PF[4][:, 0:256].rearrange("p (c h) -> p c h", c=8), [bPF[4]], [bdtf])
        act(E_, v3(0), AF.Exp, [bPF[3]], [bdtf])
        tt("dve", nb, lndt, v3(0), ALU.subtract, [bdtf, bPF[3]], [bdtf])
        tt("dve", wch, lndt, v3(256), ALU.add, [bdtf, bPF[3]], [bdtf])
        act(etot, tot, AF.Exp, [bdtf], [bdtf])
        PG = SCR[:, 4096:4096 + 2080]
        bPG = Buf("PG", aftB + aft0)
        arena["o"] = ARENA0 + 2080
        sfxb = aa(256).rearrange("p (c h) -> p c h", c=8); bsfx = Buf("sfx", aftB)
        xw = aa(1024, BF16); bxw = Buf("xw", aftB)
        bmtok = aa(256, BF16); bbmtok = Buf("bmtok", aftB)
        s.op("dve", lambda e: e.memset(sfxb[:, 7, :], 0.0), [], [bsfx])
        for c in range(6, -1, -1):
            tt("dve", sfxb[:, c, :], sfxb[:, c + 1, :], tot[:, c + 1, :], ALU.add, [bsfx, bdtf], [bsfx])
        tt("dve", PG[:, 2048:2080], sfxb[:, 0, :], tot[:, 0, :], ALU.add, [bsfx, bdtf], [bPG])
        tt("dve", sfxb, sfxb, wch, ALU.add, [bsfx, bdtf], [bsfx])
        act(wcore, sfxb, AF.Exp, [bsfx, bdtf], [bdtf])
        act(wch, wch, AF.Exp, [bdtf], [bdtf])

        for c in range(8):
            for g in range(4):
                tr(PT[1][:, g * 128:(g + 1) * 128], BK[:, g, c * 128:(c + 1) * 128], identb, [bBK[g], bcst], [bPT[1]], inc=(g == 3))
            cp("act", bmtok, PT[1][:, 0:512], [bPT[1]], [bbmtok])
            tt("pool", xw.rearrange("p (h d) -> p h d", h=32), XS[:, c, :].rearrange("p (h d) -> p h d", h=32),
               wcore[:, c, :].unsqueeze(2).broadcast_to([128, 32, 64]), ALU.mult, [bXS[c][g] for g in range(4)] + [bdtf], [bxw])
            for g in range(4):
                mm(PF[2 + g][:, 0:512], bmtok[:, g * 128:(g + 1) * 128], xw[:, g * 512:(g + 1) * 512], c == 0, c == 7,
                   [bbmtok, bxw], [bPF[2 + g]], inc=(g == 3))
        for g in range(4):
            cp("act" if g % 2 else "dve", PG[:, g * 512:(g + 1) * 512], PF[2 + g][:, 0:512], [bPF[2 + g]], [bPG])
        ba_in = Buf("a_in"); ba_out = Buf("a_out")
        dma("sp", "one", a_in.ap(), PG, [bPG], [ba_in])
        allgather("ag1", a_in, a_out, [ba_in], [ba_out])

        bsz = [Buf("sz0", aft0), Buf("sz1", aft0)]
        szv = lambda g: AUX[:, (g % 2) * 4096:(g % 2 + 1) * 4096].rearrange("p (c n) -> p c n", c=8)

        def z_evac(g):
            def ev(ps, bps, m):
                act(szv(g)[:, m, :], ps[:, 0:512], SIL, [bps], [bsz[g % 2]])
            return ev

        def v_evac(blk):
            def ev(ps, bps, m):
                cp("act" if m % 2 else "dve", VT[:, m, blk * 512:(blk + 1) * 512], ps[:, 0:512], [bps], [bVT[m][blk * 2], bVT[m][blk * 2 + 1]])
            return ev

        tok_block(512, z_evac(0))
        tok_block(512, z_evac(1))
        drain()
        tok_block(512, v_evac(0)); tok_block(512, v_evac(1)); tok_block(512, z_evac(2)); tok_block(512, v_evac(2))
        tok_block(512, z_evac(3)); tok_block(512, v_evac(3))

        SST = SCR[:, 2048:4096]
        bSST = Buf("SST", aft0)
        Lc = sm[:, 128:160]
        coef = sm[:, 160:192]
        first = True
        for j in range(6, -1, -1):
            dma("sp", "g", PG, a_out.ap()[j * 128:(j + 1) * 128, :], [ba_out], [bPG])
            if j == 6:
                s.op("dve", lambda e: e.memset(Lc, 0.0), [bsm], [bsm])
            act(coef, Lc, AF.Exp, [bsm, bcst], [bsm], bias=pc[:, 9 + j:10 + j])
            dstS = SST if first else PG[:, 0:2048]
            tt("dve", dstS.rearrange("p (h d) -> p h d", h=32), PG[:, 0:2048].rearrange("p (h d) -> p h d", h=32),
               coef.unsqueeze(2).broadcast_to([128, 32, 64]), ALU.mult, [bPG, bsm], [bSST] if first else [bPG])
            if not first:
                tt("pool" if j % 2 else "dve", SST, SST, PG[:, 0:2048], ALU.add, [bSST, bPG], [bSST])
            stt("dve", Lc, PG[:, 2048:2080], pc[:, 1 + j:2 + j], Lc, ALU.mult, ALU.add, [bPG, bcst, bsm], [bsm])
            first = False
        if debug:
            dma("sp", "dbg", dbg_d[:, 0:2048], SST, [bSST], [Buf("dbgo")])

        aftD0 = retire([bsfx, bxw, bbmtok, bPG])
        arena["o"] = ARENA0
        cb_sb = aa(128); bcb = Buf("cb", aftD0)
        dec = [aa(512).rearrange("p (h t) -> p h t", h=4) for i in range(2)]
        bdec = [Buf("dec0", aftD0), Buf("dec1", aftD0)]
        Mt = [aa(512, BF16).rearrange("p (h t) -> p h t", h=8) for i in range(2)]
        bM = [Buf("M0", aftD0), Buf("M1", aftD0)]
        t1 = aa(512); t3 = aa(512); bt1 = Buf("t1", aftD0); bt3 = Buf("t3", aftD0)
        junkb = aa(256, BF16); bjk = Buf("junkb", aftD0)
        wssdg = aa(512); bwssd = Buf("wssd", aftD0)
        yn = [aa(256, BF16), aa(256, BF16)]; byn = [Buf("yn0", aftD0), Buf("yn1", aftD0)]
        xwg = aa(256, BF16); bxwg = Buf("xwg", aftD0)
        bmt1 = aa(64, BF16); bbmt1 = Buf("bmt1", aftD0)
        Pb = [aa(256, BF16), aa(256, BF16)]; bPb = [Buf("Pb0", aftD0), Buf("Pb1", aftD0)]
        st2 = sm[:, 192:200]
        v8 = lambda ap: ap.rearrange("p (h d) -> p h d", h=8)

        def D_A(g, c):
            tok = slice(c * 128, (c + 1) * 128)
            mm(PF[4][:, 0:128], BK[:, g, tok], BK[:, 4 + g, tok], True, True, [bBK[g], bBK[4 + g]], [bPF[4]])
            cp("act", cb_sb, PF[4][:, 0:128], [bPF[4]], [bcb])
            for b in range(2):
                bank, bbank = PF[2 + b], bPF[2 + b]
                mm(bank[:, 0:512], identb, negm4, True, False, [bcst], [bbank], inc=False, sgc=True)
                for i in range(4):
                    h = g * 8 + b * 4 + i
                    mm(bank[:, i * 128:(i + 1) * 128], dA[:, c, h:h + 1].broadcast_to([128, 128]), triU, False, i == 3,
                       [bdtf, bcst], [bbank], inc=(i == 3), sgc=True)
                for i in range(4):
                    h = g * 8 + b * 4 + i
                    act(dec[b][:, i, :], bank[:, i * 128:(i + 1) * 128], AF.Exp, [bbank, bdtf], [bdec[b]], bias=nb[:, c, h:h + 1])
                tt("pool", Mt[c % 2][:, b * 4:(b + 1) * 4, :], dec[b], cb_sb.unsqueeze(1).broadcast_to([128, 4, 128]), ALU.mult,
                   [bdec[b], bcb], [bM[c % 2]])

        def D_B1(g, c):
            tok = slice(c * 128, (c + 1) * 128)
            sz = szv(g)
            for r in range(8):
                h = g * 8 + r
                mm(PF[4][:, r * 64:(r + 1) * 64], Mt[c % 2][:, r, :], XS[:, c, h * 64:(h + 1) * 64], True, True,
                   [bM[c % 2], bXS[c][g]], [bPF[4]], inc=(r == 7))
            mm(PF[5][:, 0:512], BK[:, 4 + g, tok], Pb[c % 2], True, True, [bBK[4 + g], bPb[c % 2]], [bPF[5]])
            tt("dve", v8(t1), v8(PF[5][:, 0:512]), E_[:, c, g * 8:(g + 1) * 8].unsqueeze(2).broadcast_to([128, 8, 64]), ALU.mult,
               [bPF[5], bdtf], [bt1])
            tt("dve", t1, t1, PF[4][:, 0:512], ALU.add, [bt1, bPF[4]], [bt1])

        def D_B1b(g, c):
            sz = szv(g)
            tt("pool", v8(t3), v8(XS[:, c, g * 512:(g + 1) * 512]), dskip[:, g * 8:(g + 1) * 8].unsqueeze(2).broadcast_to([128, 8, 64]),
               ALU.mult, [bXS[c][g], bcst], [bt3])
            tt("pool", t3, t3, t1, ALU.add, [bt3, bt1], [bt3])
            tt("pool", t3, t3, sz[:, c, :], ALU.mult, [bt3, bsz[g % 2]], [bt3])
            s.op("dve", lambda e: e.memset(st2[:, 0:1], 0.0), [bsm], [bsm])
            act(junkb, t3, AF.Square, [bt3, bsm], [bjk, bsm], accum=st2[:, 0:1])
            act(st2[:, 0:1], st2[:, 0:1], AF.Ln, [bsm], [bsm], bias=epsb, scale=1.0 / 512)
            act(st2[:, 0:1], st2[:, 0:1], AF.Exp, [bsm], [bsm], scale=-0.5)
            stt("dve", yn[c % 2], t3, st2[:, 0:1], wssdg, ALU.mult, ALU.mult, [bt3, bsm, bwssd], [byn[c % 2]])

        def D_C(g, c, P):
            tok = slice(c * 128, (c + 1) * 128)
            tt("pool", v8(xwg), v8(XS[:, c, g * 512:(g + 1) * 512]), wch[:, c, g * 8:(g + 1) * 8].unsqueeze(2).broadcast_to([128, 8, 64]),
               ALU.mult, [bXS[c][g], bdtf], [bxwg])
            tr(PT[1][:, 0:128], BK[:, g, tok], identb, [bBK[g], bcst], [bPT[1]])
            cp("act", bmt1, PT[1][:, 0:128], [bPT[1]], [bbmt1])
            mm(PF[5][:, 0:512], bmt1, xwg, True, True, [bbmt1, bxwg], [bPF[5]])
            tt("dve", v8(P), v8(P), etot[:, c, g * 8:(g + 1) * 8].unsqueeze(2).broadcast_to([128, 8, 64]), ALU.mult, [bSST, bdtf], [bSST])
            tt("dve", P, P, PF[5][:, 0:512], ALU.add, [bSST, bPF[5]], [bSST])
            if c < 7:
                cp("act", Pb[(c + 1) % 2], P, [bSST], [bPb[(c + 1) % 2]])

        def D_B2(g, c):
            for k in range(4):
                tr(PT[0][:, k * 128:(k + 1) * 128], yn[c % 2][:, k * 128:(k + 1) * 128], identb, [byn[c % 2], bcst], [bPT[0]], inc=(k == 3))
            cp("dve", XS[:, c, g * 512:(g + 1) * 512], PT[0][:, 0:512], [bPT[0]], [bXS[c][g]])

        it = 0
        for g in range(4):
            dma("sp", "wssd", wssdg, wssd_d[:, g * 512:(g + 1) * 512].partition_broadcast(128), [], [bwssd])
            P = SST[:, g * 512:(g + 1) * 512]
            cp("act", Pb[0], P, [bSST], [bPb[0]])
            D_A(g, 0)
            for c in range(8):
                if c < 7:
                    D_A(g, c + 1)
                D_B1(g, c)
                D_C(g, c, P)
                pump(2 if it < 24 else 0)
                D_B1b(g, c)
                if c >= 1:
                    D_B2(g, c - 1)
                it += 1
            D_B2(g, 7)
        drain()

        after = retire(bBK + [bdtf, bsz[0], bsz[1], bcb, bdec[0], bdec[1], bM[0], bM[1], bt1, bt3, bjk, bwssd, byn[0], byn[1],
                              bxwg, bbmt1, bPb[0], bPb[1], bSST, bPG])
        bKZ = [Buf("KZ%d" % c, after) for c in range(8)]
        brot = Buf("rot", after)
        rot = SCR[:, 0:2048].rearrange("p (m n) -> p m n", m=8)
        dma("sp", "one", rot, rot_d.rearrange("(m p) n -> p m n", p=128), [], [brot])
        bPG2 = Buf("PG2", after)
        PG2 = SCR[:, 4096:4096 + 2048]
        arena["o"] = ARENA0 + 2048
        ksc = aa(512); bksc = Buf("ksc", after)
        rA = aa(512); brA = Buf("rA", after)
        rB = aa(512); brB = Buf("rB", after)
        wretq = aa(512); bwretq = Buf("wretq", after)
        kzz = PG2[:, 0:256].bitcast(BF16); bkzz = bPG2
        qtok = aa(128, BF16); bqtok = Buf("qtok", after)
        o_sb = aa(512); bo = Buf("o_sb", after)
        sq = aa(512); bsq = Buf("sq", after)
        yr = [aa(256, BF16), aa(256, BF16)]; byr = [Buf("yr0", after), Buf("yr1", after)]
        sc = [ksc[:, 256 + i * 128:256 + (i + 1) * 128].bitcast(BF16).rearrange("p (h t) -> p h t", h=2) for i in range(2)]
        kzT = rA[:, 256:384].bitcast(BF16).rearrange("p (h t) -> p h t", h=2)
        PbR = [rB[:, 256:512].bitcast(BF16).rearrange("p (h e) -> p h e", h=2)] * 2
        stR = sm[:, 200:216]

        def rot_core(nh, src_scaled, m, dst, bdst):
            k4 = src_scaled.rearrange("p (h d) -> p h d", h=nh)
            tt("dve", rA[:, 0:nh * 128].rearrange("p (h d) -> p h d", h=nh), k4, rot[:, m, 0:128].unsqueeze(1).broadcast_to([128, nh, 128]),
               ALU.mult, [bksc, brot], [brA])
            rB4 = rB[:, 0:nh * 128].rearrange("p (h d) -> p h d", h=nh)
            tt("pool", rB4[:, :, 0:64], k4[:, :, 64:128], rot[:, m, 128:192].unsqueeze(1).broadcast_to([128, nh, 64]), ALU.mult, [bksc, brot], [brB])
            tt("pool", rB4[:, :, 64:128], k4[:, :, 0:64], rot[:, m, 192:256].unsqueeze(1).broadcast_to([128, nh, 64]), ALU.mult, [bksc, brot], [brB])
            tt("dve", dst, rA[:, 0:nh * 128], rB[:, 0:nh * 128], ALU.add, [brA, brB], bdst)

        kpend = []

        def k_state(blk, m):
            tt("pool", kzz.rearrange("p (h d) -> p h d", h=4), BK[:, m, blk * 512:(blk + 1) * 512].rearrange("p (h d) -> p h d", h=4),
               zz[:, m * 8 + blk * 4:m * 8 + blk * 4 + 4].unsqueeze(2).broadcast_to([128, 4, 128]), ALU.mult, [bKZ[m], bcst], [bkzz])
            for i in range(4):
                h = blk * 4 + i
                mm(PF[2 + h // 2][:, (h % 2) * 256:(h % 2 + 1) * 256], kzz[:, i * 128:(i + 1) * 128], VT[:, m, h * 256:(h + 1) * 256],
                   m == 0 and h % 2 == 0, m == 7, [bkzz, bVT[m][h]], [bPF[2 + h // 2]], inc=(i == 3), sgc=True)

        def k_evac(blk):
            def ev(ps, bps, m):
                while kpend:
                    k_state(*kpend.pop(0))
                for i in range(4):
                    act(ksc[:, i * 128:(i + 1) * 128], ps[:, i * 128:(i + 1) * 128], AF.Identity, [bps, bcst], [bksc], scale=zk[:, blk * 4 + i:blk * 4 + i + 1])
                rot_core(4, ksc, m, BK[:, m, blk * 512:(blk + 1) * 512], [bKZ[m]])
                kpend.append((blk, m))
            return ev

        tok_block(512, k_evac(0)); tok_block(512, k_evac(1))
        drain()
        while kpend:
            k_state(*kpend.pop(0))
        for b in range(4):
            cp("act" if b % 2 else "dve", PG2[:, b * 512:(b + 1) * 512], PF[2 + b][:, 0:512], [bPF[2 + b]], [bPG2])
        br_in = Buf("r_in"); br_out = Buf("r_out")
        dma("sp", "one", r_in.ap(), PG2, [bPG2], [br_in])
        allgather("ag2", r_in, r_out, [br_in], [br_out])

        SR = SCR[:, 2048:4096]; bSR = Buf("SR", after)
        sgw = AUX[:, 0:4096].rearrange("p (c n) -> p c n", c=8)
        qxT = [AUX[:, 4096:6144].rearrange("p (h t) -> p h t", h=2), AUX[:, 6144:8192].rearrange("p (h t) -> p h t", h=2)]
        bsgw = [Buf("sgw%d" % c, after) for c in range(8)]
        bqxT = [Buf("qxT0", after), Buf("qxT1", after)]
        fill_g = collections.deque()

        def q_evac(qt):
            def ev(ps, bps, m):
                for i in range(2):
                    act(ksc[:, i * 128:(i + 1) * 128], ps[:, i * 128:(i + 1) * 128], AF.Identity, [bps, bcst], [bksc], scale=xi[:, qt * 2 + i:qt * 2 + i + 1])
                rot_core(2, ksc[:, 0:256], m, qtok[:, 0:256], [bqtok])
                for i in range(2):
                    tr(PT[1][:, 512 + i * 128:512 + (i + 1) * 128], qtok[:, i * 128:(i + 1) * 128], identb, [bqtok, bcst], [bPT[1]], inc=(i == 1))
                cp("act", qxT[qt % 2][:, :, m * 128:(m + 1) * 128], PT[1][:, 512:768].rearrange("p (h t) -> p h t", h=2), [bPT[1]], [bqxT[qt % 2]])
            return ev

        def g_evac(qt):
            def ev(ps, bps, m):
                if m == 0:
                    dma("sp", "wretq", wretq, wret_d[:, qt * 512:(qt + 1) * 512].partition_broadcast(128), [], [bwretq])
                act(o_sb, ps[:, 0:512], SIL, [bps], [bo])
                tt("pool", sgw[:, m, :], o_sb, wretq, ALU.mult, [bo, bwretq], [bsgw[m]])
            return ev

        tok_block(256, q_evac(0)); tok_block(512, g_evac(0))
        drain()

        first = True
        for j in range(6, -1, -1):
            dma("sp", "g", PG2, r_out.ap()[j * 128:(j + 1) * 128, :], [br_out], [bPG2])
            dstS = SR if first else PG2
            cf = pc[:, 17 + j * 8:17 + (j + 1) * 8].unsqueeze(2).broadcast_to([128, 8, 256])
            tt("dve", dstS.rearrange("p (h e) -> p h e", h=8), PG2.rearrange("p (h e) -> p h e", h=8), cf, ALU.mult,
               [bPG2, bcst], [bSR] if first else [bPG2])
            if not first:
                tt("pool" if j % 2 else "dve", SR, SR, PG2, ALU.add, [bSR, bPG2], [bSR])
            first = False
        if debug:
            dma("sp", "dbg", dbg_d[:, 2048:4096], SR, [bSR], [Buf("dbgo2")])

        aftF = retire([bksc, brA, brB])
        bsc = [Buf("sc0", aftF), Buf("sc1", aftF)]
        bkzT = Buf("kzT", aftF)
        bPbR = [Buf("PbR", aftF)] * 2
        def F_A(qt, c):
            tok = slice(c * 128, (c + 1) * 128)
            for i in range(2):
                h = qt * 2 + i
                tr(PT[1][:, i * 128:(i + 1) * 128], BK[:, c, h * 128:(h + 1) * 128], identb, [bKZ[c], bcst], [bPT[1]], inc=(i == 1))
            cp("act", kzT, PT[1][:, 0:256].rearrange("p (h t) -> p h t", h=2), [bPT[1]], [bkzT])
            for i in range(2):
                mm(PF[2][:, i * 128:(i + 1) * 128], kzT[:, i, :], qxT[qt % 2][:, i, tok], True, True, [bkzT, bqxT[qt % 2]], [bPF[2]], inc=(i == 1))
            for i in range(2):
                h = qt * 2 + i
                stt("dve", sc[c % 2][:, i, :], PF[2][:, i * 128:(i + 1) * 128], C3[h], triU, ALU.mult, ALU.mult, [bPF[2], bcst], [bsc[c % 2]])

        def F_B1(qt, c):
            tok = slice(c * 128, (c + 1) * 128)
            for i in range(2):
                h = qt * 2 + i
                mm(PF[3][:, i * 256:(i + 1) * 256], sc[c % 2][:, i, :], VT[:, c, h * 256:(h + 1) * 256], i == 0, False, [bsc[c % 2], bVT[c][h]], [bPF[3]], inc=False, sgc=True)
            for i in range(2):
                mm(PF[3][:, i * 256:(i + 1) * 256], qxT[qt % 2][:, i, tok], PbR[c % 2][:, i, :], False, True, [bqxT[qt % 2], bPbR[c % 2]], [bPF[3]], inc=(i == 1), sgc=True)
            cp("act", o_sb, PF[3][:, 0:512], [bPF[3]], [bo])

        def F_B1b(qt, c):
            o2 = o_sb.rearrange("p (h e) -> p h e", h=2)
            s.op("dve", lambda e: e.tensor_reduce(out=stR[:, 0:2], in_=o2, axis=mybir.AxisListType.X, op=ALU.add), [bo, bsm], [bsm])
            tt("pool", sq, o_sb, o_sb, ALU.mult, [bo], [bsq])
            s.op("dve", lambda e: e.tensor_reduce(out=stR[:, 2:4], in_=sq.rearrange("p (h e) -> p h e", h=2), axis=mybir.AxisListType.X, op=ALU.add), [bsq, bsm], [bsm])
            ts("dve", stR[:, 0:2], stR[:, 0:2], 1.0 / 256, None, ALU.mult, ALU.bypass, [bsm], [bsm])
            tt("dve", stR[:, 4:6], stR[:, 0:2], stR[:, 0:2], ALU.mult, [bsm], [bsm])
            stt("dve", stR[:, 2:4], stR[:, 2:4], 1.0 / 256, stR[:, 4:6], ALU.mult, ALU.subtract, [bsm], [bsm])
            act(stR[:, 2:4], stR[:, 2:4], AF.Ln, [bsm], [bsm], bias=epsb, scale=1.0)
            act(stR[:, 2:4], stR[:, 2:4], AF.Exp, [bsm], [bsm], scale=-0.5)
            for i in range(2):
                ts("dve", sq[:, i * 256:(i + 1) * 256], o_sb[:, i * 256:(i + 1) * 256], stR[:, i:i + 1], stR[:, 2 + i:3 + i],
                   ALU.subtract, ALU.mult, [bo, bsm, bsq], [bsq])
            tt("pool", yr[c % 2], sq, sgw[:, c, :], ALU.mult, [bsq, bsgw[c]], [byr[c % 2]])

        def F_C(qt, c):
            for i in range(2):
                h = qt * 2 + i
                mm(PF[4][:, i * 256:(i + 1) * 256], BK[:, c, h * 128:(h + 1) * 128], VT[:, c, h * 256:(h + 1) * 256], True, True,
                   [bKZ[c], bVT[c][h]], [bPF[4]], inc=(i == 1))
            for i in range(2):
                h = qt * 2 + i
                Ph = SR[:, h * 256:(h + 1) * 256]
                stt("dve", Ph, Ph, G128[h], PF[4][:, i * 256:(i + 1) * 256], ALU.mult, ALU.add, [bSR, bPF[4]], [bSR])
            if c < 7:
                cp("act", PbR[(c + 1) % 2], SR[:, qt * 512:(qt + 1) * 512].rearrange("p (h e) -> p h e", h=2), [bSR], [bPbR[(c + 1) % 2]])

        def F_B2(qt, c):
            for k in range(4):
                tr(PT[0][:, k * 128:(k + 1) * 128], yr[c % 2][:, k * 128:(k + 1) * 128], identb, [byr[c % 2], bcst], [bPT[0]], inc=(k == 3))
            cp("dve", VT[:, c, qt * 512:(qt + 1) * 512], PT[0][:, 0:512], [bPT[0]], [bVT[c][qt * 2], bVT[c][qt * 2 + 1]])

        for qt in range(4):
            if qt + 1 < 4:
                tok_block(256, q_evac(qt + 1))
                tok_block(512, g_evac(qt + 1), q=fill_g, prefetch=False)
            cp("act", PbR[0], SR[:, qt * 512:(qt + 1) * 512].rearrange("p (h e) -> p h e", h=2), [bSR], [bPbR[0]])
            F_A(qt, 0)
            for c in range(8):
                if c < 7:
                    F_A(qt, c + 1)
                F_B1(qt, c)
                F_C(qt, c)
                pump(2 if c < 4 else 0)
                if c == 4 and qt + 2 < 4:
                    w_issue()
                F_B1b(qt, c)
                if c >= 2 and fill_g:
                    fill_g.popleft()()
                if c >= 1:
                    F_B2(qt, c - 1)
            F_B2(qt, 7)
            while fill_g:
                fill_g.popleft()()

        after2 = retire(buT_all + bKZ + bsgw + bqxT + bsc + bPbR + byr + [bSR, bPG2, brot, bksc, brA, brB, bqtok, bwretq, bo, bsq, bkzT, bsm, bssq, bmod])
        uTf = uT[:].rearrange("p a b -> p (a b)")[:, 0:16384].bitcast(F32)
        hT = [uTf[:, i * 2048:(i + 1) * 2048] for i in range(4)] + [BKf[:, i * 2048:(i + 1) * 2048] for i in range(2)] + \
             [AUXf[:, i * 2048:(i + 1) * 2048] for i in range(2)]
        bh = [Buf("h%d" % m, after2) for m in range(8)]
        gate_bc = SCR[:, 4096:6144]; bgate = Buf("gate", after2)
        fw_bc = SCR[:, 6176:8224]; bfw = Buf("fw", after2)
        xr_ = [SCR[:, 0:2048], SCR[:, 2048:4096]]; bxr = [Buf("xr0", after2), Buf("xr1", after2)]
        dma("sp", "one", gate_bc, modf[4096:6144].rearrange("(o n) -> o n", o=1).partition_broadcast(128), [bmout], [bgate])
        dma("sp", "one", fw_bc, wfin_d[:, :].partition_broadcast(128), [], [bfw])
        bgate.w = bfw.w
        ssqo = sm[:, 96:160]
        s.op("dve", lambda e: e.memset(ssqo, 0.0), [bsm], [bsm])
        Wflat = Wb[:].rearrange("p a b -> p (a b)")
        W4 = [Wflat[:, i * 4096:(i + 1) * 4096].rearrange("p (k n) -> p k n", k=16) for i in range(4)]
        aftW = retire(bW)
        bW4 = [Buf("W4_%d" % i, aftW) for i in range(4)]

        def wo_issue(blk):
            for half in range(2):
                sl = 2 * (blk % 2) + half
                prev = [bW4[2 * ((blk + 1) % 2)], bW4[2 * ((blk + 1) % 2) + 1]] if blk >= 1 else []
                dma("pool", "w%d" % sl, W4[sl],
                    wout_d[half * 2048:(half + 1) * 2048, blk * 256:(blk + 1) * 256].rearrange("(k p) n -> p k n", p=128), prev, [bW4[sl]])

        wo_issue(0)
        for blk in range(8):
            if blk + 1 < 8:
                wo_issue(blk + 1)
            sl0 = 2 * (blk % 2)
            for m in range(8):
                ps, bps = PF[(blk * 8 + m) % 6], bPF[(blk * 8 + m) % 6]
                for kc in range(32):
                    src = XS if kc < 16 else VT
                    k = kc % 16
                    sl = sl0 + (0 if kc < 16 else 1)
                    rd = [bXS[m][k // 4], bW4[sl]] if kc < 16 else [bVT[m][k // 2], bW4[sl]]
                    mm(ps[:, 0:256], src[:, m, k * 128:(k + 1) * 128], W4[sl][:, k, :], kc == 0, kc == 31, rd, [bps], inc=(kc == 31))
                xs_ = xr_[m % 2][:, 0:256]
                dma("sp", "xr%d" % (m % 2), xs_, x_d[m * 128:(m + 1) * 128, blk * 256:(blk + 1) * 256], [], [bxr[m % 2]])
                hh = hT[m][:, blk * 256:(blk + 1) * 256]
                tt("dve", hh, ps[:, 0:256], gate_bc[:, blk * 256:(blk + 1) * 256], ALU.mult, [bps, bgate], [bh[m]])
                tt("pool", hh, hh, xs_, ALU.add, [bh[m], bxr[m % 2]], [bh[m]])
                act(xr_[m % 2][:, 512:768], hh, AF.Square, [bh[m], bxr[m % 2]], [bxr[m % 2], bsm], accum=ssqo[:, m * 8 + blk:m * 8 + blk + 1])
        s.op("dve", lambda e: e.tensor_reduce(out=ssq[:, 0:8], in_=ssqo.rearrange("p (m n) -> p m n", m=8), axis=mybir.AxisListType.X, op=ALU.add), [bsm], [bsm])
        act(ssq[:, 0:8], ssq[:, 0:8], AF.Ln, [bsm], [bsm], bias=epsb, scale=1.0 / D)
        act(ssq[:, 0:8], ssq[:, 0:8], AF.Exp, [bsm], [bsm], scale=-0.5)
        bout = Buf("out")
        for m in range(8):
            if m % 3 != 2:
                stt("dve", hT[m], hT[m], ssq[:, m:m + 1], fw_bc, ALU.mult, ALU.mult, [bh[m], bsm, bfw], [bh[m]])
            else:
                act(hT[m], hT[m], AF.Identity, [bh[m], bsm], [bh[m]], scale=ssq[:, m:m + 1])
                tt("pool", hT[m], hT[m], fw_bc, ALU.mult, [bh[m], bfw], [bh[m]])
            dma("sp", "out", out_d[m * 128:(m + 1) * 128, :], hT[m], [bh[m]], [bout])
        if debug:
            s.wait("sp", (DS("dbg").key, DS("dbg").count))
        s.wait("sp", (DS("out").key, DS("out").count))
        s.emit(nc)
    return nc


def _consts():
    p = np.arange(128)
    cst = np.zeros((128, NCST), np.float32)
    cst[:, 0:128] = np.eye(128, dtype=np.float32)
    cst[:, 128:256] = (p[:, None] <= p[None, :]).astype(np.float32)
    cst[:, 256:384] = (p[:, None] > p[None, :]).astype(np.float32)
    cst[:, 384:512] = 1.0
    lg = np.array(LG, np.float64)
    cst[:, 512:520] = (128.0 ** -0.5) * np.exp((127 - p)[:, None] * lg[None, :])
    cst[:, 520:528] = np.exp((p + 1.0)[:, None] * lg[None, :])
    zz = np.exp(128.0 * (7 - np.arange(8))[:, None] * lg[None, :])
    cst[:, 528:592] = zz.reshape(1, 64)
    cstb = np.zeros((128, 640), np.float32)
    cstb[:, 0:128] = np.eye(128, dtype=np.float32)
    negm = np.where(p[:, None] > p[None, :], -30000.0, 0.0).astype(np.float32)
    cstb[:, 128:640] = np.tile(negm, (1, 4))
    return cst, cstb


def _percore(i):
    lg = np.array(LG, np.float64)
    pc = np.zeros((128, NPC), np.float32)
    pc[:, 0] = 0.0 if i == 0 else 1.0
    for j in range(8):
        f = 1.0 if j < i else 0.0
        pc[:, 1 + j] = f
        pc[:, 9 + j] = (f - 1.0) * 30000.0
    for j in range(7):
        for h in range(8):
            pc[:, 17 + j * 8 + h] = math.exp(1024.0 * (i - 1 - j) * lg[h]) if j < i else 0.0
    pos = (i * T + np.arange(T)).astype(np.float32)
    inv = (10000.0 ** (-(np.arange(64, dtype=np.float32)) / 64.0)).astype(np.float32)
    ang = pos[:, None] * inv[None, :]
    cos = np.cos(ang).astype(np.float32)
    sin = np.sin(ang).astype(np.float32)
    rot = np.concatenate([cos, cos, -sin, sin], axis=1).astype(np.float32)
    return pc, rot


_NC_CACHE = {}


def kernel(x, c, w_ada, b_ada, norm_w, w_in, conv_w, conv_b, dt_bias, a_log, d_skip,
           ssd_norm_w, ret_norm_w, w_out, final_norm_w, _debug=None):
    f = lambda a: np.ascontiguousarray(np.asarray(a, dtype=np.float32))
    x2 = f(x)[0]
    cst, cstb = _consts()
    idx = np.arange(2048).reshape(16, 128).T.reshape(-1)
    perm = np.concatenate([idx, 2048 + idx, 4096 + np.arange(2048)])
    wa = f(w_ada)[0][:, perm]
    ba = f(b_ada)[0][perm]
    pp = np.zeros((128, 152), np.float32)
    pp[:, 0:16] = f(norm_w)[0].reshape(16, 128).T
    pp[:, 16:32] = f(c)[0].reshape(16, 128).T
    pp[:, 32:128] = f(conv_w)[0].T.reshape(24, 128, 4).transpose(1, 0, 2).reshape(128, 96)
    pp[:, 128:152] = f(conv_b)[0].reshape(24, 128).T
    bv = np.concatenate([f(dt_bias)[0], f(a_log)[0], f(d_skip)[0]])[None, :]
    win = f(w_in)[0]
    wout = f(w_out)[0]
    common = {"w_in": win, "w_out": wout, "cst": cst, "cstb": cstb, "pp": pp, "bv": f(bv),
              "ssd_norm_w": f(ssd_norm_w)[0][None, :], "ret_norm_w": f(ret_norm_w)[0][None, :],
              "final_norm_w": f(final_norm_w)[None, :]}
    in_maps = []
    for i in range(NCORE):
        pc, rot = _percore(i)
        xh = np.zeros((3, D), np.float32) if i == 0 else x2[i * T - 3:i * T]
        d = dict(common)
        d.update({"x": x2[i * T:(i + 1) * T], "xh": f(xh), "w_ada": f(wa[:, i * 768:(i + 1) * 768]),
                  "b_ada": f(ba[i * 768:(i + 1) * 768])[None, :], "pc": pc, "rot": rot})
        in_maps.append(d)
    if _debug is not None:
        res = run_bass_kernel_spmd(build_program(debug=_debug), in_maps, core_ids=list(range(NCORE)))
        _debug[2].extend(res.results[i]["dbg"] for i in range(NCORE))
    else:
        if "nc" not in _NC_CACHE:
            _NC_CACHE["nc"] = build_program()
        res = run_bass_kernel_spmd(_NC_CACHE["nc"], in_maps, core_ids=list(range(NCORE)))
    out = np.concatenate([res.results[i]["out"] for i in range(NCORE)], axis=0)
    return out[None].astype(np.float32)
```
